# Optimizing a Trainium2 kernel written in Bass

```python
import jax
import jax.numpy as jnp
from jax import lax
import numpy as np

D_MODEL = 2048
BATCH = 4
SEQ = 8192
DEPTH = 1

A_HEADS = 8
A_HEAD_DIM = 128
A_WIDTH = A_HEADS * A_HEAD_DIM
MOBA_BLOCK = 256
MOBA_TOPK = 3
MOBA_Q_CHUNK = 16
ROPE_THETA = 500000.0
ROPE_DIM = A_HEAD_DIM // 4
R_HEADS = 8
R_KEY_DIM = 128
R_VAL_DIM = 128
R_KWIDTH = R_HEADS * R_KEY_DIM
R_VWIDTH = R_HEADS * R_VAL_DIM
R_CHUNK = 64
N_BRANCH = 2
D_FF = 4 * D_MODEL
D_IN = 3 * A_WIDTH + 2 * R_KWIDTH + 2 * R_VWIDTH + N_BRANCH * D_MODEL
EPS = 1e-6

kernel_name = "hybrid_moba_hgrn2_gated_merge_block"


def _rmsnorm(x, g):
    x32 = x.astype(jnp.float32)
    y = x32 * lax.rsqrt(jnp.mean(x32 * x32, axis=-1, keepdims=True) + EPS)
    return y.astype(x.dtype) * g


def _rope_tables(seq, dtype):
    pos = jnp.arange(seq, dtype=jnp.float32)
    inv_freq = ROPE_THETA ** (-jnp.arange(0, ROPE_DIM, 2, dtype=jnp.float32) / ROPE_DIM)
    ang = pos[:, None] * inv_freq[None, :]
    return jnp.cos(ang).astype(dtype), jnp.sin(ang).astype(dtype)


def _partial_rope(x, cos, sin):
    half = ROPE_DIM // 2
    x1 = x[..., :half]
    x2 = x[..., half:ROPE_DIM]
    return jnp.concatenate([x1 * cos - x2 * sin, x2 * cos + x1 * sin, x[..., ROPE_DIM:]], axis=-1)


def _moba_attention(q, k, v):
    bsz, nh, seq, hd = q.shape
    s_pad = -(-seq // MOBA_BLOCK) * MOBA_BLOCK
    pad = ((0, 0), (0, 0), (0, s_pad - seq), (0, 0))
    q = jnp.pad(q, pad)
    k = jnp.pad(k, pad)
    v = jnp.pad(v, pad)
    nb = s_pad // MOBA_BLOCK
    kb = k.reshape(bsz, nh, nb, MOBA_BLOCK, hd)
    vb = v.reshape(bsz, nh, nb, MOBA_BLOCK, hd)
    k_mean = jnp.mean(kb.astype(jnp.float32), axis=3)
    gate = jnp.einsum('bhsd,bhnd->bhsn', q.astype(jnp.float32), k_mean)
    q_pos = jnp.arange(s_pad)
    q_blk = q_pos // MOBA_BLOCK
    past = jnp.arange(nb)[None, :] < q_blk[:, None]
    gate = jnp.where(past, gate, -jnp.inf)
    n_top = min(MOBA_TOPK, nb)
    _, top_idx = lax.top_k(gate, n_top)
    top_valid = top_idx < q_blk[:, None]
    own = jnp.broadcast_to(q_blk[:, None], (bsz, nh, s_pad, 1)).astype(top_idx.dtype)
    idx = jnp.concatenate([top_idx, own], axis=-1)
    valid = jnp.concatenate([top_valid, jnp.ones((bsz, nh, s_pad, 1), dtype=bool)], axis=-1)
    n_sel = n_top + 1
    n_chunks = s_pad // MOBA_Q_CHUNK

    def to_chunks(t):
        return jnp.moveaxis(t.reshape(bsz, nh, n_chunks, MOBA_Q_CHUNK, *t.shape[3:]), 2, 0)

    b_ix = jnp.arange(bsz)[:, None, None, None]
    h_ix = jnp.arange(nh)[None, :, None, None]
    offs = jnp.arange(MOBA_BLOCK)
    scale = hd ** -0.5

    def attend(args):
        q_c, idx_c, valid_c, pos_c = args
        k_sel = kb[b_ix, h_ix, idx_c]
        v_sel = vb[b_ix, h_ix, idx_c]
        key_pos = idx_c[..., None] * MOBA_BLOCK + offs
        mask = valid_c[..., None] & (key_pos <= pos_c[None, None, :, None, None])
        logits = jnp.einsum('bhqd,bhqnld->bhqnl', q_c, k_sel, preferred_element_type=jnp.float32) * scale
        logits = jnp.where(mask, logits, -jnp.inf).reshape(bsz, nh, MOBA_Q_CHUNK, n_sel * MOBA_BLOCK)
        p = jax.nn.softmax(logits, axis=-1).reshape(bsz, nh, MOBA_Q_CHUNK, n_sel, MOBA_BLOCK)
        return jnp.einsum('bhqnl,bhqnld->bhqd', p.astype(v_sel.dtype), v_sel)

    out = lax.map(attend, (to_chunks(q), to_chunks(idx), to_chunks(valid), q_pos.reshape(n_chunks, MOBA_Q_CHUNK)))
    out = jnp.moveaxis(out, 0, 2).reshape(bsz, nh, s_pad, hd)
    return out[:, :, :seq]


def _hgrn2(q, f_pre, i, lb):
    bsz, seq, _ = q.shape
    n_chunks = seq // R_CHUNK

    def to_chunks(t, d):
        return t.astype(jnp.float32).reshape(bsz, n_chunks, R_CHUNK, R_HEADS, d).transpose(1, 0, 3, 2, 4)

    qc = to_chunks(jax.nn.silu(q.astype(jnp.float32)), R_KEY_DIM)
    lb_h = lb.reshape(R_HEADS, R_KEY_DIM)[None, None, :, None, :]
    f = lb_h + (1.0 - lb_h) * jax.nn.sigmoid(to_chunks(f_pre, R_KEY_DIM))
    log_f = jnp.log(f)
    kc = 1.0 - f
    vc = to_chunks(i, R_VAL_DIM)
    causal = jnp.tril(jnp.ones((R_CHUNK, R_CHUNK), dtype=bool))[:, :, None]

    def step(state, inp):
        q_t, k_t, v_t, lf_t = inp
        b = jnp.cumsum(lf_t, axis=2)
        diff = b[:, :, :, None, :] - b[:, :, None, :, :]
        decay = jnp.exp(jnp.where(causal, diff, -jnp.inf))
        scores = jnp.einsum('bhtsd,bhsd->bhts', q_t[:, :, :, None, :] * decay, k_t)
        o = jnp.einsum('bhts,bhsv->bhtv', scores, v_t) + jnp.einsum('bhtd,bhdv->bhtv', q_t * jnp.exp(b), state)
        b_last = b[:, :, -1:, :]
        state = jnp.exp(b_last[:, :, 0, :])[..., None] * state + jnp.einsum('bhsd,bhsv->bhdv', k_t * jnp.exp(b_last - b), v_t)
        return state, o

    state0 = jnp.zeros((bsz, R_HEADS, R_KEY_DIM, R_VAL_DIM), jnp.float32)
    _, o = lax.scan(step, state0, (qc, kc, vc, log_f))
    return o.transpose(1, 0, 3, 2, 4).reshape(bsz, seq, R_HEADS, R_VAL_DIM)


def _layer(x, w_in, b_gate, attn_norm_g, q_norm_g, k_norm_g, lb, hgrn_out_norm_g, w_up_a, w_up_r, w_out, mlp_norm_g, w_mlp1, w_mlp2):
    bsz, seq, _ = x.shape
    h = _rmsnorm(x, attn_norm_g)
    proj = h @ w_in
    sizes = [A_WIDTH] * 3 + [R_KWIDTH] * 2 + [R_VWIDTH] * 2 + [D_MODEL] * N_BRANCH
    q_a, k_a, v_a, q_r, f_r, i_r, g_r, gate_a, gate_r = jnp.split(proj, np.cumsum(sizes)[:-1].tolist(), axis=-1)

    def heads(t):
        return t.reshape(bsz, seq, A_HEADS, A_HEAD_DIM)

    cos, sin = _rope_tables(seq, x.dtype)
    q_a = _partial_rope(_rmsnorm(heads(q_a), q_norm_g).transpose(0, 2, 1, 3), cos, sin)
    k_a = _partial_rope(_rmsnorm(heads(k_a), k_norm_g).transpose(0, 2, 1, 3), cos, sin)
    v_a = heads(v_a).transpose(0, 2, 1, 3)
    o_a = _moba_attention(q_a, k_a, v_a).transpose(0, 2, 1, 3).reshape(bsz, seq, A_WIDTH)

    o_r = _hgrn2(q_r, f_r, i_r, lb)
    g_r = jax.nn.sigmoid(g_r.astype(jnp.float32).reshape(bsz, seq, R_HEADS, R_VAL_DIM))
    o_r = (_rmsnorm(o_r, hgrn_out_norm_g.astype(jnp.float32)) * g_r).reshape(bsz, seq, R_VWIDTH).astype(x.dtype)

    merged = jax.nn.sigmoid(gate_a + b_gate[0]) * (o_a @ w_up_a) + jax.nn.sigmoid(gate_r + b_gate[1]) * (o_r @ w_up_r)
    x = x + merged @ w_out
    u = jax.nn.relu(_rmsnorm(x, mlp_norm_g) @ w_mlp1)
    return x + (u * u) @ w_mlp2


def setup_inputs(seed: int = 0) -> dict:
    key = jax.random.key(seed)
    ks = jax.random.split(key, 16)
    f32 = jnp.float32

    def nrm(k, shape, fan_in):
        return jax.random.normal(k, shape, f32) * (fan_in ** -0.5)

    def gain(k, shape):
        return 1.0 + 0.02 * jax.random.normal(k, shape, f32)

    return {
        "x": jax.random.normal(ks[0], (BATCH, SEQ, D_MODEL), f32),
        "w_in": nrm(ks[1], (DEPTH, D_MODEL, D_IN), D_MODEL),
        "b_gate": 0.02 * jax.random.normal(ks[2], (DEPTH, N_BRANCH, D_MODEL), f32),
        "attn_norm_g": gain(ks[3], (DEPTH, D_MODEL)),
        "q_norm_g": gain(ks[4], (DEPTH, A_HEAD_DIM)),
        "k_norm_g": gain(ks[5], (DEPTH, A_HEAD_DIM)),
        "hgrn_lb_logits": 0.5 * jax.random.normal(ks[6], (DEPTH + 1, R_KWIDTH), f32),
        "hgrn_out_norm_g": gain(ks[7], (DEPTH, R_VAL_DIM)),
        "w_up_a": nrm(ks[8], (DEPTH, A_WIDTH, D_MODEL), A_WIDTH),
        "w_up_r": nrm(ks[9], (DEPTH, R_VWIDTH, D_MODEL), R_VWIDTH),
        "w_out": nrm(ks[10], (DEPTH, D_MODEL, D_MODEL), D_MODEL),
        "mlp_norm_g": gain(ks[11], (DEPTH, D_MODEL)),
        "w_mlp1": nrm(ks[12], (DEPTH, D_MODEL, D_FF), D_MODEL),
        "w_mlp2": nrm(ks[13], (DEPTH, D_FF, D_MODEL), D_FF),
    }


def reference(x, w_in, b_gate, attn_norm_g, q_norm_g, k_norm_g, hgrn_lb_logits, hgrn_out_norm_g, w_up_a, w_up_r, w_out, mlp_norm_g, w_mlp1, w_mlp2):
    lb_all = jnp.cumsum(jax.nn.softmax(hgrn_lb_logits.astype(jnp.float32), axis=0), axis=0)
    for l in range(DEPTH):
        x = _layer(x, w_in[l], b_gate[l], attn_norm_g[l], q_norm_g[l], k_norm_g[l], lb_all[l], hgrn_out_norm_g[l], w_up_a[l], w_up_r[l], w_out[l], mlp_norm_g[l], w_mlp1[l], w_mlp2[l])
    return x
```

```python
import numpy as np
from contextlib import ExitStack
import concourse.bass as bass
import concourse.mybir as mybir
from concourse.bass_utils import run_bass_kernel_spmd

F32 = mybir.dt.float32
BF16 = mybir.dt.bfloat16
U8 = mybir.dt.uint8
AF = mybir.ActivationFunctionType
ALU = mybir.AluOpType
AX = mybir.AxisListType


class _Op:
    __slots__ = ("eng", "fn", "chan", "grp", "deps", "sig", "cnt", "idx")


class Prog:
    ENGS = ("pe", "act", "dve", "pool", "sp")

    def __init__(self):
        self.ops = []
        self.last_w = {}
        self.readers = {}
        self.chan_ops = {}

    def add(self, eng, fn, reads=(), writes=(), chan=None, grp=None):
        op = _Op()
        op.eng, op.fn, op.chan, op.grp = eng, fn, chan, grp
        op.sig, op.cnt = False, 0
        i = op.idx = len(self.ops)
        deps = set()
        reads = tuple(reads) + ("__all__",)
        for t in reads:
            j = self.last_w.get(t)
            if j is not None:
                deps.add(j)
        for t in writes:
            j = self.last_w.get(t)
            if j is not None:
                deps.add(j)
            rd = self.readers.get(t)
            if rd:
                deps.update(rd.values())
        for t in writes:
            self.last_w[t] = i
            self.readers[t] = {}
        key = ("c", chan) if chan is not None else eng
        for t in reads:
            self.readers.setdefault(t, {})[key] = i
        deps.discard(i)
        if chan is not None:
            deps = {j for j in deps if not (self.ops[j].chan == chan and self.ops[j].grp == grp)}
        op.deps = deps
        if chan is not None:
            self.chan_ops.setdefault(chan, []).append(i)
        self.ops.append(op)
        return i

    def barrier(self, dma_fn):
        self.add("sp", dma_fn, reads=(), writes=("__all__",), chan="__bar__", grp=len(self.chan_ops.get("__bar__", ())))

    def emit(self, nc, es):
        ops = self.ops
        for op in ops:
            for j in op.deps:
                d = ops[j]
                if d.chan is None and not (d.eng == "pe" and op.eng == "pe" and op.chan is None):
                    d.sig = True
        cnt = {e: 0 for e in self.ENGS}
        for op in ops:
            if op.chan is None and op.sig:
                cnt[op.eng] += 1
                op.cnt = cnt[op.eng]
        chan_final = {}
        for ch, lst in self.chan_ops.items():
            last = {}
            for n, i in enumerate(lst):
                last[ops[i].grp] = 16 * (n + 1)
            for i in lst:
                chan_final[i] = last[ops[i].grp]
        sems = {e: es.enter_context(nc.semaphore("s_" + e)) for e in self.ENGS}
        csems = {ch: es.enter_context(nc.semaphore("c_%d" % n)) for n, ch in enumerate(self.chan_ops)}
        self.n_sems = len(sems) + len(csems)
        per_eng = {e: [] for e in self.ENGS}
        for op in ops:
            per_eng[op.eng].append(op)
        block = es.enter_context(nc.Block())
        engobj = {"pe": (block.tensor, nc.tensor), "act": (block.scalar, nc.scalar), "dve": (block.vector, nc.vector),
                  "pool": (block.gpsimd, nc.gpsimd), "sp": (block.sync, nc.sync)}
        final_waits = {}
        for ch, lst in self.chan_ops.items():
            final_waits[ch] = 16 * len(lst)

        def make(ename):
            def body(eng):
                known = {}
                for op in per_eng[ename]:
                    need = {}
                    for j in op.deps:
                        d = ops[j]
                        if d.chan is not None:
                            k, v = ("c", d.chan), chan_final[j]
                        else:
                            if d.eng == "pe" and ename == "pe" and op.chan is None:
                                continue
                            k, v = d.eng, d.cnt
                        if v > need.get(k, 0):
                            need[k] = v
                    for k, v in need.items():
                        if v > known.get(k, 0):
                            known[k] = v
                            eng.wait_ge(csems[k[1]] if isinstance(k, tuple) else sems[k], v)
                    ins = op.fn(eng)
                    if op.chan is not None:
                        ins.then_inc(csems[op.chan], 16)
                    elif op.sig:
                        ins.then_inc(sems[ename], 1)
                if ename == "sp":
                    for ch, v in final_waits.items():
                        if v > known.get(("c", ch), 0):
                            eng.wait_ge(csems[ch], v)
            return body

        for e in self.ENGS:
            engobj[e][0](make(e))


class Arena:
    def __init__(self, ap, nbytes):
        self.ap, self.nbytes, self.off = ap, nbytes, 0

    def reset(self, off=0):
        self.off = off

    def alloc(self, dtype, *free):
        n = int(np.prod(free))
        size = {F32: 4, BF16: 2, U8: 1}[dtype]
        off = (self.off + 63) // 64 * 64
        nb = n * size
        assert off + nb <= self.nbytes, ("SBUF arena overflow", off + nb, self.nbytes)
        self.off = off + nb
        v = self.ap[:, off:off + nb]
        if dtype != U8:
            v = v.bitcast(dtype)
        if len(free) == 2:
            v = v.rearrange("p (a b) -> p a b", a=free[0])
        elif len(free) == 3:
            v = v.rearrange("p (a b c) -> p a b c", a=free[0], b=free[1])
        return v


CFG = dict(D=2048, H=8, T=4096, TT=512)
EPS = 1e-6
NEG = -1.0e30
ARENA_BYTES = 206 * 1024


def _dims(cfg):
    D, H, T, TT = cfg["D"], cfg["H"], cfg["T"], cfg["TT"]
    d = dict(D=D, H=H, T=T, TT=TT, KC=D // 128, AW=H * 128, DFF=4 * D, FC=4 * D // 128, NS=TT // 128,
             NT=T // TT, TALL=2 * T, NBK=T // 256, DIN=7 * H * 128 + 2 * D)
    d["KU2"] = min(16, d["FC"])
    d["CWU"] = 512 if (D % 512 == 0 and H * 512 <= 4096) else 256
    return d


def build(cfg):
    dm = _dims(cfg)
    D, H, T, TT, KC, AW, DFF, FC, NS, NT, TALL, NBK, DIN = (dm[k] for k in
        ("D", "H", "T", "TT", "KC", "AW", "DFF", "FC", "NS", "NT", "TALL", "NBK", "DIN"))
    KU2, CWU = dm["KU2"], dm["CWU"]
    NBA = 2 * NBK
    NBAP = max(NBA, 8)
    nc = bass.Bass("TRN2", target_bir_lowering=False)
    P = Prog()

    def din(name, shape):
        return nc.dram_tensor(name, list(shape), F32, kind="ExternalInput").ap()

    def dscr(name, shape, dt=BF16):
        return nc.dram_tensor(name, list(shape), dt).ap()

    x_own, x_pre = din("x_own", [T, D]), din("x_pre", [T, D])
    w_in, w_ua, w_ur = din("w_in", [D, DIN]), din("w_up_a", [AW, D]), din("w_up_r", [AW, D])
    w_o, w_1, w_2 = din("w_out", [D, D]), din("w_mlp1", [D, DFF]), din("w_mlp2", [DFF, D])
    g_attn, g_mlp = din("g_attn", [D]), din("g_mlp", [D])
    bg_in = din("bgate", [128, 2 * KC])
    qk_g = din("qk_g", [128, 2])
    lbl_in = din("lbl", [128, 2 * H])
    gn_in = din("gn", [128])
    c_ident, c_mask2 = din("c_ident", [128, 128]), din("c_mask2", [128, 128])
    c_cm, c_m512, c_perm = din("c_cm", [128, 512]), din("c_m512", [128, TT]), din("c_perm", [32, 32])
    cos_in, sin_in = din("cosT", [32, TALL]), din("sinT", [32, TALL])
    vb_in, vm_in = din("validb", [128, NBK * NBAP]), din("validm", [128, NBK * NBAP])
    y = nc.dram_tensor("y", [T, D], F32, kind="ExternalOutput").ap()

    NU_IN, NU_1, NU_O = DIN // 256, DFF // 256, D // 256
    NU_2 = (D // 256) * (FC // KU2)
    NU_U = D // CWU
    ws_in = dscr("ws_in", [NU_IN, 128, KC * 256])
    ws_1 = dscr("ws_1", [NU_1, 128, KC * 256])
    ws_o = dscr("ws_o", [NU_O, 128, KC * 256])
    ws_2 = dscr("ws_2", [NU_2, 128, KU2 * 256])
    ws_ua = dscr("ws_ua", [NU_U, 128, H * CWU])
    ws_ur = dscr("ws_ur", [NU_U, 128, H * CWU])
    QTs, KTs = dscr("QTs", [H, 128, T]), dscr("KTs", [H, 128, TALL])
    Vs = dscr("Vs", [H, TALL, 128])
    ORTs, OATs = dscr("ORTs", [H, 128, T]), dscr("OATs", [H, 128, T])

    es = ExitStack()
    arena_t = es.enter_context(nc.sbuf_tensor("arena", [128, ARENA_BYTES], U8))
    A = Arena(arena_t, ARENA_BYTES)
    PS = [es.enter_context(nc.psum_tensor("ps%d" % i, [128, 512], F32)) for i in range(8)]
    PSB = [p[:, :].bitcast(BF16) for p in PS]

    def act(out, in_, func, r, w, **kw):
        P.add("act", lambda e: e.activation(out=out, in_=in_, func=func, **kw), r, w)

    def tt(eng, out, in0, in1, op, r, w):
        P.add(eng, lambda e: e.tensor_tensor(out=out, in0=in0, in1=in1, op=op), r, w)

    def ts(eng, out, in0, s1, s2, op0, op1, r, w):
        P.add(eng, lambda e: e.tensor_scalar(out=out, in0=in0, scalar1=s1, scalar2=s2, op0=op0, op1=op1), r, w)

    def stt(out, in0, sc, in1, op0, op1, r, w):
        P.add("dve", lambda e: e.scalar_tensor_tensor(out=out, in0=in0, scalar=sc, in1=in1, op0=op0, op1=op1), r, w)

    def cp(eng, out, in_, r, w):
        if eng == "act":
            P.add("act", lambda e: e.activation(out=out, in_=in_, func=AF.Copy), r, w)
        else:
            P.add(eng, lambda e: e.tensor_copy(out=out, in_=in_), r, w)

    def mm(out, lhsT, rhs, start, stop, r, w):
        P.add("pe", lambda e: e.matmul(out, lhsT=lhsT, rhs=rhs, start=start, stop=stop), r, w)

    def dma(out, in_, r, w, chan, grp):
        P.add("sp", lambda e: e.dma_start(out=out, in_=in_), r, w, chan=chan, grp=grp)

    def red(out, in_, r, w, op=ALU.add):
        P.add("dve", lambda e: e.tensor_reduce(out=out, in_=in_, axis=AX.X, op=op), r, w)

    cnt = {}

    def nxt(name):
        cnt[name] = cnt.get(name, -1) + 1
        return cnt[name]

    def tr(out, in_, r, w):
        P.add("pe", lambda e: e.transpose(out=out, in_=in_, identity=idb), list(r) + ["idb"], w)

    idf = A.alloc(F32, 128)
    idb = A.alloc(BF16, 128)
    ones_b = A.alloc(BF16, 128)
    gA = A.alloc(F32, D)
    gM = A.alloc(F32, D)
    bg = A.alloc(F32, 2 * KC)
    qkg = A.alloc(F32, 2)
    lbl = A.alloc(F32, 2 * H)
    lb = A.alloc(F32, H)
    oml = A.alloc(F32, H)
    gnb = A.alloc(F32, 128)
    mask2 = A.alloc(F32, 128)
    cmf = A.alloc(F32, 512)
    cmb = A.alloc(BF16, 2, 256)
    m512 = A.alloc(F32, TT)
    perm = A.alloc(F32, 32)
    vbb = A.alloc(F32, NBK, NBAP)
    vmm = A.alloc(F32, NBK, NBAP)
    dummy = A.alloc(F32, 16)
    dummy2 = A.alloc(F32, 16)
    for n_, (dst, src) in enumerate([(idf, c_ident), (gA, g_attn.partition_broadcast(128)), (gM, g_mlp.partition_broadcast(128)),
                                     (bg, bg_in), (qkg, qk_g), (lbl, lbl_in), (gnb, gn_in.partition_broadcast(128)),
                                     (mask2, c_mask2), (cmf, c_cm), (m512, c_m512), (perm[0:32, :], c_perm),
                                     (vbb, vb_in.rearrange("p (j n) -> p j n", n=NBAP)), (vmm, vm_in.rearrange("p (j n) -> p j n", n=NBAP))]):
        dma(dst, src, [], ["const%d" % n_], "const", 0)
    CONST = ["const%d" % i for i in range(13)]
    cp("dve", idb, idf, CONST, ["idb"])
    P.add("pool", lambda e: e.memset(ones_b, 1.0), [], ["ones_b"])
    cp("dve", cmb, cmf.rearrange("p (a b) -> p a b", a=2), CONST, ["cmb"])
    tt("dve", lb, lbl[:, 0:H], lbl[:, H:2 * H], ALU.subtract, CONST, ["lb"])
    act(lb, lb, AF.Sigmoid, ["lb"], ["lb"])
    ts("dve", oml, lb, -1.0, 1.0, ALU.mult, ALU.add, ["lb"], ["oml"])
    P.add("pool", lambda e: e.memset(dummy, 0.0), [], ["dummy"])
    const_end = A.off

    def barrier():
        P.barrier(lambda e: e.dma_start(out=dummy2, in_=dummy))

    def prepass():
        A.reset(const_end)
        sf = [A.alloc(F32, 4096) for _ in range(2)]
        sb = [A.alloc(BF16, 4096) for _ in range(2)]
        engs = ["act", "dve", "pool"]
        jobs = []

        def add_w(src, dst, K, N, KU, CW):
            ng = K // (128 * KU)
            for cg in range(N // CW):
                for kg in range(ng):
                    jobs.append((src[kg * KU * 128:(kg + 1) * KU * 128, cg * CW:(cg + 1) * CW].rearrange("(k p) n -> p k n", p=128),
                                 dst[cg * ng + kg], KU, CW))
        add_w(w_in, ws_in, D, DIN, KC, 256)
        add_w(w_ua, ws_ua, AW, D, H, CWU)
        add_w(w_ur, ws_ur, AW, D, H, CWU)
        add_w(w_o, ws_o, D, D, KC, 256)
        add_w(w_1, ws_1, D, DFF, KC, 256)
        add_w(w_2, ws_2, DFF, D, KU2, 256)
        for n, (src, dst, KU, CW) in enumerate(jobs):
            s = n % 2
            f = sf[s][:, 0:KU * CW]
            b = sb[s][:, 0:KU * CW]
            dma(f.rearrange("p (k n) -> p k n", k=KU), src, [], ["sf%d" % s], ("pl", s), n)
            cp(engs[n % 3], b, f, ["sf%d" % s], ["sb%d" % s])
            dma(dst, b, ["sb%d" % s], [], ("ps", s), n)

    wring = {}

    def wring_init(nslots):
        wring["slots"] = [A.alloc(BF16, 4096) for _ in range(nslots)]
        wring["n"] = 0

    def wload(scr, u, KU, CW):
        n = wring["n"]
        wring["n"] += 1
        s = n % len(wring["slots"])
        t = wring["slots"][s][:, 0:KU * CW]
        dma(t, scr[u], [], ["w%d" % s], ("w", s), nxt("wgrp"))
        return t.rearrange("p (k n) -> p k n", k=KU), "w%d" % s

    def norm_transpose(xrow, xtok, gvec, junk, hb, hT, s, tag, stat):
        act(junk, xrow, AF.Square, [xtok], ["junk"])
        red(stat[:, 0:1], junk, ["junk"], ["stat"])
        ts("dve", stat[:, 1:2], stat[:, 0:1], 1.0 / D, EPS, ALU.mult, ALU.add, ["stat"], ["stat"])
        act(stat[:, 2:3], stat[:, 1:2], AF.Sqrt, ["stat"], ["stat"])
        P.add("dve", lambda e: e.reciprocal(out=stat[:, 3:4], in_=stat[:, 2:3]), ["stat"], ["stat"])
        act(junk, xrow, AF.Copy, [xtok, "stat"], ["junk"], scale=stat[:, 3:4])
        tt("pool", hb, junk, gvec, ALU.mult, ["junk"] + CONST, ["hb"])
        for half in range((KC + 7) // 8):
            k0, k1 = half * 8, min(KC, half * 8 + 8)
            bank = nxt("trbank") % 2
            pv = PSB[bank][:, 0:(k1 - k0) * 128].rearrange("p (k t) -> p k t", t=128)
            for kc in range(k0, k1):
                tr(pv[:, kc - k0, :], hb[:, kc * 128:(kc + 1) * 128], ["hb"], ["ps%d" % bank])
            cp("dve" if half % 2 == 0 else "act", hT[:, k0:k1, s * 128:(s + 1) * 128], pv, ["ps%d" % bank], [tag])

    def phaseA():
        A.reset(const_end)
        wring_init(4)
        xs = [A.alloc(F32, D) for _ in range(2)]
        junk = A.alloc(F32, D)
        hb = A.alloc(BF16, D)
        hT = A.alloc(BF16, KC, TT)
        stat = A.alloc(F32, 4)
        pqs, t1, qn = A.alloc(F32, TT), A.alloc(F32, TT), A.alloc(F32, TT)
        sq = A.alloc(BF16, TT)
        ta, tb = A.alloc(F32, TT), A.alloc(F32, TT)
        cs = [A.alloc(F32, TT), A.alloc(F32, TT)]
        QTst = [A.alloc(BF16, 2, TT) for _ in range(2)]
        Vst = [A.alloc(BF16, NS, 256) for _ in range(2)]
        Fh, LF, LB, BB, E1, E2, SQ, Kt = (A.alloc(F32, TT) for _ in range(8))
        kTt = A.alloc(BF16, 2, TT)
        Ap, Bp = A.alloc(BF16, 2, TT), A.alloc(BF16, 2, TT)
        C0, C1, C2, CT = (A.alloc(F32, 2, 2 * NS) for _ in range(4))
        vtok = A.alloc(BF16, NS, 256)
        gg = A.alloc(F32, NS, 256)
        NUu = 2 * NS
        kA, kB = A.alloc(BF16, NUu, 128), A.alloc(BF16, NUu, 128)
        Tds = A.alloc(F32, 2 * NUu, 128)
        Sst = A.alloc(F32, 2, 2 * NS + 1, 128)
        Sper = A.alloc(F32, H, 128)
        Stb = A.alloc(BF16, 2 * NUu, 128)
        scm = A.alloc(BF16, NUu, 128)
        junk2 = A.alloc(F32, NUu, 128)
        otmp = A.alloc(F32, NUu, 128)
        ssq2 = A.alloc(F32, 4, NUu)
        orb = A.alloc(BF16, NUu, 128)
        ORTst = [A.alloc(BF16, 2, TT) for _ in range(2)]
        P.add("pool", lambda e: e.memset(kA, 0.0), [], ["kA"])
        P.add("pool", lambda e: e.memset(kB, 0.0), [], ["kB"])
        P.add("pool", lambda e: e.memset(Ap, 0.0), [], ["Ap"])
        P.add("pool", lambda e: e.memset(Bp, 0.0), [], ["Bp"])
        P.add("pool", lambda e: e.memset(Sper, 0.0), [], ["Sper"])
        MB = [4, 5, 6, 7]
        PB = [2, 3]

        def mbank():
            return MB[nxt("mb") % 4]

        def proj_fm(u, cc, wt, wtok):
            b = PB[nxt("pb") % 2]
            for kc in range(KC):
                mm(PS[b][:, 0:TT], wt[:, kc, cc * 128:(cc + 1) * 128], hT[:, kc, :], kc == 0, kc == KC - 1,
                   [wtok, "hT"], ["ps%d" % b])
            return b

        def proj_tm(wt, wtok):
            for s in range(NS):
                b = PB[s // 2]
                for kc in range(KC):
                    mm(PS[b][:, (s % 2) * 256:(s % 2) * 256 + 256], hT[:, kc, s * 128:(s + 1) * 128], wt[:, kc, :],
                       kc == 0, kc == KC - 1, [wtok, "hT"], ["ps%d" % b])

        def tm_view(s0, n):
            return PS[PB[s0 // 2]][:, (s0 % 2) * 256:(s0 % 2) * 256 + n * 256].rearrange("p (s c) -> p s c", c=256)

        def qk_post(b, gcol, tok0, st, hh):
            pb = "ps%d" % b
            act(pqs, PS[b][:, 0:TT], AF.Copy, [pb], ["pqs"])
            act(sq, PS[b][:, 0:TT], AF.Square, [pb], ["sq"])
            m = mbank()
            mm(PS[m][:, 0:TT], ones_b, sq, True, True, ["sq", "ones_b"], ["ps%d" % m])
            ts("dve", t1, PS[m][:, 0:TT], 1.0 / 128, EPS, ALU.mult, ALU.add, ["ps%d" % m], ["t1"])
            act(t1, t1, AF.Sqrt, ["t1"], ["t1"])
            P.add("dve", lambda e: e.reciprocal(out=t1, in_=t1), ["t1"], ["t1"])
            stt(qn, pqs, gcol, t1, ALU.mult, ALU.mult, ["pqs", "t1"] + CONST, ["qn"])
            cp("pool", st[:, hh, :], qn, ["qn"], [st_tok[0]])
            m = mbank()
            mm(PS[m][0:32, 0:TT], perm[0:32, :], qn[0:32, :], True, True, ["qn"] + CONST, ["ps%d" % m])
            tt("dve", ta[0:32, :], qn[0:32, :], cs[0][0:32, :], ALU.mult, ["qn", "cs0"], ["ta"])
            tt("dve", tb[0:32, :], PS[m][0:32, 0:TT], cs[1][0:32, :], ALU.mult, ["ps%d" % m, "cs1"], ["tb"])
            tt("pool", st[0:32, hh, :], ta[0:32, :], tb[0:32, :], ALU.add, ["ta", "tb"], [st_tok[0]])

        st_tok = [None]
        n_in = dict(qa=0, ka=AW // 256, va=2 * AW // 256, qr=3 * AW // 256, fr=4 * AW // 256, ir=5 * AW // 256, gr=6 * AW // 256)

        for ti in range(2 * NT):
            own = ti >= NT
            xsrc = x_own if own else x_pre
            r0 = (ti - NT if own else ti) * TT
            tok0 = ti * TT
            q0 = r0
            for s in range(NS):
                n = nxt("xs")
                xt = xs[n % 2]
                dma(xt, xsrc[r0 + s * 128:r0 + (s + 1) * 128, :], [], ["xs%d" % (n % 2)], ("xs", n % 2), n)
                norm_transpose(xt, "xs%d" % (n % 2), gA, junk, hb, hT, s, "hT", stat)
            dma(cs[0][0:32, :], cos_in[:, tok0:tok0 + TT], [], ["cs0"], "cs", ti)
            dma(cs[1][0:32, :], sin_in[:, tok0:tok0 + TT], [], ["cs1"], "cs", ti)
            for grp, gidx, scr, soff in ([("qa", 0, QTs, q0)] if own else []) + [("ka", 1, KTs, tok0)]:
                for up in range(AW // 256):
                    wt, wtok = wload(ws_in, n_in[grp] + up, KC, 256)
                    n = nxt("QTst")
                    st = QTst[n % 2]
                    st_tok[0] = "QTst%d" % (n % 2)
                    for cc in range(2):
                        b = proj_fm(up, cc, wt, wtok)
                        qk_post(b, qkg[:, gidx:gidx + 1], tok0, st, cc)
                    dma(scr[2 * up:2 * up + 2, :, soff:soff + TT].rearrange("h p t -> p h t"), st, [st_tok[0]],
                        [], ("QTst", n % 2), n)
            for up in range(AW // 256):
                wt, wtok = wload(ws_in, n_in["va"] + up, KC, 256)
                proj_tm(wt, wtok)
                n = nxt("Vst")
                vs_ = Vst[n % 2]
                for hf in range((NS + 1) // 2):
                    nn = min(2, NS - 2 * hf)
                    cp("act" if hf == 0 else "dve", vs_[:, 2 * hf:2 * hf + nn, :], tm_view(2 * hf, nn), ["ps%d" % PB[hf]], ["Vst%d" % (n % 2)])
                for hh in range(2):
                    dma(Vs[2 * up + hh, tok0:tok0 + TT, :].rearrange("(s p) d -> p s d", p=128), vs_[:, :, hh * 128:(hh + 1) * 128],
                        ["Vst%d" % (n % 2)], [], ("Vst", n % 2), n)
            for up in range(AW // 256):
                wt, wtok = wload(ws_in, n_in["fr"] + up, KC, 256)
                bf = [proj_fm(up, 0, wt, wtok), None]
                for hh in range(2):
                    h = 2 * up + hh
                    if hh == 1:
                        bf[1] = proj_fm(up, 1, wt, wtok)
                    pb = "ps%d" % bf[hh]
                    act(Fh, PS[bf[hh]][:, 0:TT], AF.Sigmoid, [pb], ["Fh"])
                    ts("dve", Fh, Fh, oml[:, h:h + 1], lb[:, h:h + 1], ALU.mult, ALU.add, ["Fh", "lb", "oml"], ["Fh"])
                    act(LF, Fh, AF.Ln, ["Fh"], ["LF"])
                    P.add("dve", lambda e: e.tensor_tensor_scan(out=LB, data0=m512, data1=LF, initial=0.0, op0=ALU.mult, op1=ALU.add),
                          ["LF"] + CONST, ["LB"])
                    LB3 = LB.rearrange("p (c t) -> p c t", t=64)
                    tt("dve", BB.rearrange("p (c t) -> p c t", t=64), LB3, LB3[:, :, 31:32].broadcast_to([128, 2 * NS, 64]), ALU.subtract, ["LB"], ["BB"])
                    act(E2, BB, AF.Exp, ["BB"], ["E2"], scale=-1.0)
                    ts("pool", Kt, Fh, -1.0, 1.0, ALU.mult, ALU.add, ["Fh"], ["Kt"])
                    tt("dve", kTt[:, hh, :], Kt, E2, ALU.mult, ["Kt", "E2"], ["kTt"])
                    act(C0[:, hh, :], LB3[:, :, 31], AF.Exp, ["LB"], ["C0"])
                    act(C1[:, hh, :], LB3[:, :, 63], AF.Exp, ["LB"], ["C1"])
                    tt("dve", CT[:, hh, :], LB3[:, :, 63], LB3[:, :, 31], ALU.subtract, ["LB"], ["CT"])
                    act(C2[:, hh, :], CT[:, hh, :], AF.Exp, ["CT"], ["C2"])
                    if own:
                        act(E1, BB, AF.Exp, ["BB"], ["E1"])
                        if hh == 0:
                            wq, wqtok = wload(ws_in, n_in["qr"] + up, KC, 256)
                        bq = proj_fm(up, hh, wq, wqtok)
                        act(SQ, PS[bq][:, 0:TT], AF.Silu, ["ps%d" % bq], ["SQ"])
                        SQ4 = SQ.rearrange("p (c t) -> p c t", t=128)
                        E14 = E1.rearrange("p (c t) -> p c t", t=128)
                        A4 = Ap[:, hh, :].rearrange("p (c t) -> p c t", t=128)
                        B4 = Bp[:, hh, :].rearrange("p (c t) -> p c t", t=128)
                        tt("dve", A4[:, :, 0:64], SQ4[:, :, 0:64], E14[:, :, 0:64], ALU.mult, ["SQ", "E1"], ["Ap"])
                        tt("pool", B4[:, :, 64:128], SQ4[:, :, 64:128], E14[:, :, 64:128], ALU.mult, ["SQ", "E1"], ["Bp"])
                wt, wtok = wload(ws_in, n_in["ir"] + up, KC, 256)
                proj_tm(wt, wtok)
                for hf in range((NS + 1) // 2):
                    nn = min(2, NS - 2 * hf)
                    cp("act" if hf == 0 else "dve", vtok[:, 2 * hf:2 * hf + nn, :], tm_view(2 * hf, nn), ["ps%d" % PB[hf]], ["vtok"])
                if own:
                    wt, wtok = wload(ws_in, n_in["gr"] + up, KC, 256)
                    proj_tm(wt, wtok)
                    for hf in range((NS + 1) // 2):
                        nn = min(2, NS - 2 * hf)
                        act(gg[:, 2 * hf:2 * hf + nn, :], tm_view(2 * hf, nn), AF.Sigmoid, ["ps%d" % PB[hf]], ["gg"])
                    tt("pool", gg.rearrange("p s (h v) -> p (s h) v", v=128), gg.rearrange("p s (h v) -> p (s h) v", v=128),
                       gnb.unsqueeze(1).broadcast_to([128, 2 * NS, 128]), ALU.mult, ["gg"] + CONST, ["gg"])
                m = mbank()
                pv = PSB[m][:, 0:NUu * 128].rearrange("p (u d) -> p u d", d=128)
                for hh in range(2):
                    for sc in range(NS):
                        tr(pv[:, hh * NS + sc, :], kTt[:, hh, sc * 128:(sc + 1) * 128], ["kTt"], ["ps%d" % m])
                cp("act", kA[0:64, :, :], pv[0:64, :, :], ["ps%d" % m], ["kA"])
                cp("dve", kB[64:128, :, :], pv[64:128, :, :], ["ps%d" % m], ["kB"])
                for hh in range(2):
                    h = 2 * up + hh
                    cp("pool", Sst[:, hh, 0, :], Sper[:, h, :], ["Sper"], ["Sst"])
                for hh in range(2):
                    for sc in range(NS):
                        u = hh * NS + sc
                        m = mbank()
                        vv = vtok[:, sc, hh * 128:(hh + 1) * 128]
                        mm(PS[m][:, 0:128], kA[:, u, :], vv, True, True, ["kA", "vtok"], ["ps%d" % m])
                        mm(PS[m][:, 128:256], kB[:, u, :], vv, True, True, ["kB", "vtok"], ["ps%d" % m])
                        for j in range(2):
                            c = 2 * sc + j
                            act(Tds[:, 2 * u + j, :], PS[m][:, j * 128:(j + 1) * 128], AF.Copy, ["ps%d" % m, "C2"], ["Tds"], scale=C2[:, hh, c:c + 1])
                            stt(Sst[:, hh, c + 1, :], Sst[:, hh, c, :], C1[:, hh, c:c + 1], Tds[:, 2 * u + j, :], ALU.mult, ALU.add,
                                ["Sst", "Tds", "C1"], ["Sst"])
                            if own:
                                ts("pool", Stb[:, 2 * u + j, :], Sst[:, hh, c, :], C0[:, hh, c:c + 1], None, ALU.mult, ALU.bypass, ["Sst", "C0"], ["Stb"])
                for hh in range(2):
                    h = 2 * up + hh
                    cp("pool", Sper[:, h, :], Sst[:, hh, 2 * NS, :], ["Sst"], ["Sper"])
                if not own:
                    continue
                for half in range((NUu + 3) // 4):
                    m = mbank()
                    u0, u1 = half * 4, min(NUu, half * 4 + 4)
                    for u in range(u0, u1):
                        hh, sc = divmod(u, NS)
                        o_ = PS[m][:, (u - u0) * 128:(u - u0 + 1) * 128]
                        kk = kTt[:, hh, sc * 128:(sc + 1) * 128]
                        mm(o_, kk, Ap[:, hh, sc * 128:(sc + 1) * 128], True, False, ["kTt", "Ap"], ["ps%d" % m])
                        mm(o_, kk, Bp[:, hh, sc * 128:(sc + 1) * 128], False, True, ["kTt", "Bp"], ["ps%d" % m])
                    tt("dve", scm[:, u0:u1, :], PS[m][:, 0:(u1 - u0) * 128].rearrange("p (u t) -> p u t", t=128),
                       mask2.unsqueeze(1).broadcast_to([128, u1 - u0, 128]), ALU.mult, ["ps%d" % m] + CONST, ["scm"])
                for half in range((NUu + 3) // 4):
                    m = mbank()
                    u0, u1 = half * 4, min(NUu, half * 4 + 4)
                    nu = u1 - u0
                    for u in range(u0, u1):
                        hh, sc = divmod(u, NS)
                        o_ = PS[m][:, (u - u0) * 128:(u - u0 + 1) * 128]
                        mm(o_, scm[:, u, :], vtok[:, sc, hh * 128:(hh + 1) * 128], True, False, ["scm", "vtok"], ["ps%d" % m])
                        mm(o_, Ap[:, hh, sc * 128:(sc + 1) * 128], Stb[:, 2 * u, :], False, False, ["Ap", "Stb"], ["ps%d" % m])
                        mm(o_, Bp[:, hh, sc * 128:(sc + 1) * 128], Stb[:, 2 * u + 1, :], False, True, ["Bp", "Stb"], ["ps%d" % m])
                    pm = "ps%d" % m
                    ov = PS[m][:, 0:nu * 128].rearrange("p (u v) -> p u v", v=128)
                    act(junk2[:, u0:u1, :], ov, AF.Square, [pm], ["junk2"])
                    red(ssq2[:, 0, u0:u1], junk2[:, u0:u1, :], ["junk2"], ["ssq2"])
                    ts("dve", ssq2[:, 1, u0:u1], ssq2[:, 0, u0:u1], 1.0 / 128, EPS, ALU.mult, ALU.add, ["ssq2"], ["ssq2"])
                    act(ssq2[:, 2, u0:u1], ssq2[:, 1, u0:u1], AF.Sqrt, ["ssq2"], ["ssq2"])
                    P.add("dve", lambda e, u0=u0, u1=u1: e.reciprocal(out=ssq2[:, 3, u0:u1], in_=ssq2[:, 2, u0:u1]), ["ssq2"], ["ssq2"])
                    tt("dve", otmp[:, u0:u1, :], ov, ssq2[:, 3, u0:u1].unsqueeze(2).broadcast_to([128, nu, 128]), ALU.mult, [pm, "ssq2"], ["otmp"])
                    for u in range(u0, u1):
                        hh, sc = divmod(u, NS)
                        tt("pool", orb[:, u, :], otmp[:, u, :], gg[:, sc, hh * 128:(hh + 1) * 128], ALU.mult, ["otmp", "gg"], ["orb"])
                m = mbank()
                pv = PSB[m][:, 0:NUu * 128].rearrange("p (u d) -> p u d", d=128)
                for u in range(NUu):
                    tr(pv[:, u, :], orb[:, u, :], ["orb"], ["ps%d" % m])
                n = nxt("ORTst")
                ot = ORTst[n % 2]
                cp("act", ot.rearrange("p h (s t) -> p (h s) t", t=128), pv, ["ps%d" % m], ["ORTst%d" % (n % 2)])
                dma(ORTs[2 * up:2 * up + 2, :, q0:q0 + TT].rearrange("h p t -> p h t"), ot, ["ORTst%d" % (n % 2)], [], ("ORTst", n % 2), n)

    def phaseB():
        A.reset(const_end)
        NKT = TALL // 128
        KTh = [A.alloc(BF16, TALL) for _ in range(2)]
        Vh = [A.alloc(BF16, NKT, 129) for _ in range(2)]
        QTh = [A.alloc(BF16, T) for _ in range(2)]
        OAst = [A.alloc(BF16, T) for _ in range(2)]
        km = A.alloc(F32, NBAP)
        kmb = A.alloc(BF16, NBAP)
        gsb = A.alloc(F32, 2, NBAP)
        thr8 = A.alloc(F32, 2, 8)
        sel = A.alloc(F32, 2, NBAP)
        negc = A.alloc(F32, 1)
        PT = [A.alloc(BF16, 2, 256) for _ in range(3)]
        acc = A.alloc(F32, 2, 129)
        rec = A.alloc(F32, 2)
        oab = A.alloc(BF16, 2, 128)
        P.add("pool", lambda e: e.memset(negc, -float(np.sqrt(128.0))), [], ["negc"])
        P.add("pool", lambda e: e.memset(km, 0.0), [], ["km"])
        for i in range(2):
            P.add("pool", lambda e, i=i: e.memset(Vh[i][:, :, 128:129], 1.0), [], ["Vone%d" % i])
        SB, OB, GB, TB = [0, 1, 2], [3, 4], 5, 6
        scale = 128.0 ** -0.5
        for h in range(H):
            i = h % 2
            dma(KTh[i], KTs[h], [], ["KTh%d" % i], ("KTh", i), h)
            for j0 in range(0, NKT, 8):
                j1 = min(NKT, j0 + 8)
                dma(Vh[i][:, j0:j1, 0:128], Vs[h, j0 * 128:j1 * 128, :].rearrange("(j p) d -> p j d", p=128), [], ["Vh%d" % i], ("Vh", i), h)
            dma(QTh[i], QTs[h], [], ["QTh%d" % i], ("QTh", i), h)
            red(km[:, 0:NBA], KTh[i].rearrange("p (n k) -> p n k", k=256), ["KTh%d" % i], ["km"])
            ts("dve", kmb, km, 1.0 / 256, None, ALU.mult, ALU.bypass, ["km"], ["kmb"])
            for j in range(NBK):
                ncand = NBK + j
                qsl = QTh[i][:, j * 256:(j + 1) * 256]
                gv = PS[GB][:, 0:2 * NBAP].rearrange("p (q n) -> p q n", n=NBAP)
                for qs in range(2):
                    mm(gv[:, qs, :], qsl[:, qs * 128:(qs + 1) * 128], kmb, True, True, ["QTh%d" % i, "kmb"], ["ps%d" % GB])
                tt("dve", gsb, gv, vbb[:, j, :].unsqueeze(1).broadcast_to([128, 2, NBAP]), ALU.add, ["ps%d" % GB] + CONST, ["gsb"])
                for qs in range(2):
                    P.add("dve", lambda e, qs=qs: e.max(out=thr8[:, qs, :], in_=gsb[:, qs, :]), ["gsb"], ["thr8"])
                    ts("dve", sel[:, qs, :], gsb[:, qs, :], thr8[:, qs, 2:3], None, ALU.is_ge, ALU.bypass, ["gsb", "thr8"], ["sel"])
                tt("dve", sel, sel, vmm[:, j, :].unsqueeze(1).broadcast_to([128, 2, NBAP]), ALU.mult, ["sel"] + CONST, ["sel"])
                P.add("pool", lambda e: e.memset(acc, 0.0), [], ["acc"])
                for n in range(ncand + 1):
                    ownb = n == ncand
                    sb_ = SB[nxt("sb") % 3]
                    for kt in range(2):
                        mm(PS[sb_][:, kt * 256:(kt + 1) * 256], KTh[i][:, n * 256 + kt * 128:n * 256 + (kt + 1) * 128], qsl, True, True,
                           ["KTh%d" % i, "QTh%d" % i], ["ps%d" % sb_])
                    pn = nxt("PT") % 3
                    pt = PT[pn]
                    act(pt, PS[sb_][:, :].rearrange("p (k q) -> p k q", q=256), AF.Exp, ["ps%d" % sb_, "negc"], ["PT%d" % pn], scale=scale, bias=negc[:, 0:1])
                    if ownb:
                        tt("pool", pt, pt, cmb, ALU.mult, ["PT%d" % pn, "cmb"], ["PT%d" % pn])
                    ob = OB[nxt("ob") % 2]
                    for qs in range(2):
                        for kt in range(2):
                            mm(PS[ob][:, qs * 129:(qs + 1) * 129], pt[:, kt, qs * 128:(qs + 1) * 128], Vh[i][:, n * 2 + kt, :], kt == 0, kt == 1,
                               ["PT%d" % pn, "Vh%d" % i, "Vone%d" % i], ["ps%d" % ob])
                    for qs in range(2):
                        o_ = PS[ob][:, qs * 129:(qs + 1) * 129]
                        if ownb:
                            tt("dve", acc[:, qs, :], o_, acc[:, qs, :], ALU.add, ["ps%d" % ob, "acc"], ["acc"])
                        else:
                            stt(acc[:, qs, :], o_, sel[:, qs, n:n + 1], acc[:, qs, :], ALU.mult, ALU.add, ["ps%d" % ob, "acc", "sel"], ["acc"])
                P.add("dve", lambda e: e.reciprocal(out=rec, in_=acc[:, :, 128]), ["acc"], ["rec"])
                for qs in range(2):
                    ts("dve", oab[:, qs, :], acc[:, qs, 0:128], rec[:, qs:qs + 1], None, ALU.mult, ALU.bypass, ["acc", "rec"], ["oab"])
                pv = PSB[TB][:, 0:256].rearrange("p (q t) -> p q t", t=128)
                for qs in range(2):
                    tr(pv[:, qs, :], oab[:, qs, :], ["oab"], ["ps%d" % TB])
                cp("act", OAst[i][:, j * 256:(j + 1) * 256], PSB[TB][:, 0:256], ["ps%d" % TB], ["OAst%d" % i])
            dma(OATs[h], OAst[i], ["OAst%d" % i], [], ("OAst", i), h)

    def phaseC():
        A.reset(const_end)
        wring_init(4)
        xt = A.alloc(F32, NS, D)
        junk = A.alloc(F32, D)
        hb = A.alloc(BF16, D)
        hT = A.alloc(BF16, KC, TT)
        stat = A.alloc(F32, 4)
        oaT, orT = A.alloc(BF16, H, TT), A.alloc(BF16, H, TT)
        gsA, gsR, m1, m2 = (A.alloc(F32, TT) for _ in range(4))
        mT = A.alloc(BF16, KC, TT)
        FCH = FC // 2
        uT = A.alloc(BF16, FCH, TT)
        rr = [A.alloc(F32, TT) for _ in range(2)]
        PB = [2, 3]
        n_ga, n_gb = 7 * AW // 256, 7 * AW // 256 + D // 256
        CPU = CWU // 128
        for ti in range(NT):
            q0 = ti * TT
            for s in range(NS):
                dma(xt[:, s, :], x_own[q0 + s * 128:q0 + (s + 1) * 128, :], [], ["xt"], "xt", 2 * ti)
            for s in range(NS):
                norm_transpose(xt[:, s, :], "xt", gA, junk, hb, hT, s, "hT", stat)
            dma(oaT, OATs[:, :, q0:q0 + TT].rearrange("h p t -> p h t"), [], ["oaT"], "oaT", ti)
            dma(orT, ORTs[:, :, q0:q0 + TT].rearrange("h p t -> p h t"), [], ["orT"], "orT", ti)
            for c in range(KC):
                if c % 2 == 0:
                    wga, tga = wload(ws_in, n_ga + c // 2, KC, 256)
                    wgb, tgb = wload(ws_in, n_gb + c // 2, KC, 256)
                if c % CPU == 0:
                    wua, tua = wload(ws_ua, c // CPU, H, CWU)
                    wur, tur = wload(ws_ur, c // CPU, H, CWU)
                cc, cu = c % 2, c % CPU
                for (wg, tg, wu, tu, gs, mo, br) in ((wga, tga, wua, tua, gsA, m1, 0), (wgb, tgb, wur, tur, gsR, m2, 1)):
                    b = PB[nxt("pbc") % 2]
                    for kc in range(KC):
                        mm(PS[b][:, 0:TT], wg[:, kc, cc * 128:(cc + 1) * 128], hT[:, kc, :], kc == 0, kc == KC - 1, [tg, "hT"], ["ps%d" % b])
                    act(gs, PS[b][:, 0:TT], AF.Sigmoid, ["ps%d" % b] + CONST, ["gs%d" % br], bias=bg[:, br * KC + c:br * KC + c + 1])
                    b = PB[nxt("pbc") % 2]
                    src = oaT if br == 0 else orT
                    for hh in range(H):
                        mm(PS[b][:, 0:TT], wu[:, hh, cu * 128:(cu + 1) * 128], src[:, hh, :], hh == 0, hh == H - 1,
                           [tu, "oaT" if br == 0 else "orT"], ["ps%d" % b])
                    tt("dve", mo, PS[b][:, 0:TT], gs, ALU.mult, ["ps%d" % b, "gs%d" % br], ["m%d" % br])
                tt("pool", mT[:, c, :], m1, m2, ALU.add, ["m0", "m1"], ["mT"])
            for uo in range(D // 256):
                wo, two = wload(ws_o, uo, KC, 256)
                for s in range(NS):
                    b = 4 + s % 4
                    for kc in range(KC):
                        mm(PS[b][:, 0:256], mT[:, kc, s * 128:(s + 1) * 128], wo[:, kc, :], kc == 0, kc == KC - 1, [two, "mT"], ["ps%d" % b])
                    xv = xt[:, s, uo * 256:(uo + 1) * 256]
                    tt("dve", xv, PS[b][:, 0:256], xv, ALU.add, ["ps%d" % b, "xt", "hb", "junk"], ["xt"])
            for s in range(NS):
                norm_transpose(xt[:, s, :], "xt", gM, junk, hb, hT, s, "hT", stat)
            for half in range(2):
                for fu in range(NU_1 // 2):
                    w1t, t1 = wload(ws_1, half * (NU_1 // 2) + fu, KC, 256)
                    for cc in range(2):
                        b = PB[nxt("pbc") % 2]
                        for kc in range(KC):
                            mm(PS[b][:, 0:TT], w1t[:, kc, cc * 128:(cc + 1) * 128], hT[:, kc, :], kc == 0, kc == KC - 1, [t1, "hT"], ["ps%d" % b])
                        rn = nxt("rr") % 2
                        act(rr[rn], PS[b][:, 0:TT], AF.Relu, ["ps%d" % b], ["rr%d" % rn])
                        tt("pool", uT[:, fu * 2 + cc, :], rr[rn], rr[rn], ALU.mult, ["rr%d" % rn], ["uT"])
                ngk = FC // KU2
                kgs = range(half * ngk // 2, (half + 1) * ngk // 2) if ngk >= 2 else range(0, 1)
                for cg in range(D // 256):
                    if ngk >= 2:
                        units = [(kg, 0, KU2) for kg in kgs]
                    else:
                        units = [(0, half * FCH, (half + 1) * FCH)]
                    nmm = sum(k1 - k0 for (_, k0, k1) in units)
                    done = [0] * NS
                    for (kg, k0, k1) in units:
                        w2t, t2 = wload(ws_2, cg * ngk + kg, KU2, 256)
                        for s in range(NS):
                            b = 4 + s % 4
                            for k in range(k0, k1):
                                fcl = (kg * KU2 + k) - half * FCH
                                mm(PS[b][:, 0:256], uT[:, fcl, s * 128:(s + 1) * 128], w2t[:, k, :], done[s] == 0, done[s] == nmm - 1,
                                   [t2, "uT"], ["ps%d" % b])
                                done[s] += 1
                    for s in range(NS):
                        b = 4 + s % 4
                        xv = xt[:, s, cg * 256:(cg + 1) * 256]
                        tt("dve", xv, PS[b][:, 0:256], xv, ALU.add, ["ps%d" % b, "xt"], ["xt"])
            dma(y[q0:q0 + TT, :].rearrange("(s p) d -> p s d", p=128), xt, ["xt"], [], "xt", 2 * ti + 1)

    prepass()
    barrier()
    phaseA()
    barrier()
    phaseB()
    barrier()
    phaseC()
    P.emit(nc, es)
    es.close()
    return nc, P


def host_inputs(cfg, core, x, w_in, b_gate, attn_norm_g, q_norm_g, k_norm_g, hgrn_lb_logits, hgrn_out_norm_g,
                w_up_a, w_up_r, w_out, mlp_norm_g, w_mlp1, w_mlp2, rope_theta=500000.0):
    dm = _dims(cfg)
    D, H, T, TT, KC, NBK = dm["D"], dm["H"], dm["T"], dm["TT"], dm["KC"], dm["NBK"]
    NBA = 2 * NBK
    NBAP = max(NBA, 8)
    f = np.float32
    b, half = core // 2, core % 2
    xb = np.asarray(x[b], dtype=f)
    m = {}
    m["x_own"] = np.ascontiguousarray(xb[half * T:(half + 1) * T])
    m["x_pre"] = np.ascontiguousarray(xb[0:T]) if half == 1 else np.zeros((T, D), f)
    m["w_in"], m["w_up_a"], m["w_up_r"] = (np.ascontiguousarray(np.asarray(a[0], dtype=f)) for a in (w_in, w_up_a, w_up_r))
    m["w_out"], m["w_mlp1"], m["w_mlp2"] = (np.ascontiguousarray(np.asarray(a[0], dtype=f)) for a in (w_out, w_mlp1, w_mlp2))
    m["g_attn"] = np.ascontiguousarray(np.asarray(attn_norm_g[0], dtype=f))
    m["g_mlp"] = np.ascontiguousarray(np.asarray(mlp_norm_g[0], dtype=f))
    m["bgate"] = np.ascontiguousarray(np.asarray(b_gate[0], dtype=f).reshape(2, KC, 128).transpose(2, 0, 1).reshape(128, 2 * KC))
    m["qk_g"] = np.ascontiguousarray(np.stack([np.asarray(q_norm_g[0], f), np.asarray(k_norm_g[0], f)], axis=1))
    m["lbl"] = np.ascontiguousarray(np.asarray(hgrn_lb_logits, dtype=f).reshape(2, H, 128).transpose(2, 0, 1).reshape(128, 2 * H))
    m["gn"] = np.ascontiguousarray(np.asarray(hgrn_out_norm_g[0], dtype=f))
    m["c_ident"] = np.eye(128, dtype=f)
    s_ = np.arange(128)
    m["c_mask2"] = ((s_[:, None] // 64 == s_[None, :] // 64) & (s_[:, None] <= s_[None, :])).astype(f)
    q_ = np.arange(256)
    m["c_cm"] = np.stack([(kt * 128 + s_[:, None] <= q_[None, :]) for kt in range(2)], axis=1).astype(f).reshape(128, 512)
    m["c_m512"] = np.tile((np.arange(TT) % 64 != 0).astype(f)[None, :], (128, 1))
    pm = np.zeros((32, 32), f)
    for mm_ in range(32):
        pm[(mm_ + 16) % 32, mm_] = 1.0
    m["c_perm"] = pm
    pos = np.concatenate([np.arange(T), half * T + np.arange(T)]).astype(f)
    inv_freq = (f(rope_theta) ** (-np.arange(0, 32, 2, dtype=f) / f(32))).astype(f)
    ang = (pos[None, :] * inv_freq[:, None]).astype(f)
    c, s = np.cos(ang).astype(f), np.sin(ang).astype(f)
    m["cosT"] = np.ascontiguousarray(np.concatenate([c, c], axis=0))
    m["sinT"] = np.ascontiguousarray(np.concatenate([-s, s], axis=0))
    vb = np.full((NBK, NBAP), NEG, f)
    for j in range(NBK):
        if half == 1:
            vb[j, 0:NBK] = 0.0
        vb[j, NBK:NBK + j] = 0.0
    vm = (vb == 0.0).astype(f)
    m["validb"] = np.ascontiguousarray(np.tile(vb.reshape(1, -1), (128, 1)))
    m["validm"] = np.ascontiguousarray(np.tile(vm.reshape(1, -1), (128, 1)))
    return m


_CACHE = {}


def kernel(**inputs):
    cfg = CFG
    key = tuple(sorted(cfg.items()))
    if key not in _CACHE:
        _CACHE[key] = build(cfg)[0]
    nc = _CACHE[key]
    T = cfg["T"]
    x = np.asarray(inputs["x"])
    B = x.shape[0]
    in_maps = [host_inputs(cfg, c, **inputs) for c in range(2 * B)]
    res = run_bass_kernel_spmd(nc, in_maps, core_ids=list(range(2 * B)))
    out = np.empty(x.shape, np.float32)
    for c in range(2 * B):
        out[c // 2, (c % 2) * T:(c % 2 + 1) * T] = np.asarray(res.results[c]["y"], dtype=np.float32)
    return out
```
